# Optimizing a Trainium2 kernel written in Bass

```python
import math
import jax, jax.numpy as jnp
from jax import lax
import numpy as np

D_MODEL = 1024
BATCH = 8
SEQ = 8192
DEPTH = 1

N_META = 16
N_ATTN_HEADS = 16
QK_NOPE_DIM = 64
QK_ROPE_DIM = 32
QK_HEAD_DIM = QK_NOPE_DIM + QK_ROPE_DIM
V_HEAD_DIM = 64
Q_LORA_RANK = 384
KV_LORA_RANK = 256
ROPE_THETA = 10000.0
ATTN_WIDTH = N_ATTN_HEADS * V_HEAD_DIM
Q_BLOCK = 128
SSD_EXPAND = 2
SSD_WIDTH = SSD_EXPAND * D_MODEL
SSD_HEAD_DIM = 64
N_SSD_HEADS = SSD_WIDTH // SSD_HEAD_DIM
N_SSD_GROUPS = 4
D_STATE = 128
CONV_WIDTH = 5
CONV_DIM = SSD_WIDTH + 2 * N_SSD_GROUPS * D_STATE
CHUNK = 128
N_BRANCHES = 2
NORM_EPS = 1e-6
IN_SPLITS = (Q_LORA_RANK, KV_LORA_RANK, QK_ROPE_DIM, ATTN_WIDTH, SSD_WIDTH, CONV_DIM, 2 * N_SSD_HEADS, N_BRANCHES * D_MODEL)
IN_DIM = Q_LORA_RANK + KV_LORA_RANK + QK_ROPE_DIM + ATTN_WIDTH + SSD_WIDTH + CONV_DIM + 2 * N_SSD_HEADS + N_BRANCHES * D_MODEL

kernel_name = "hybrid_mla_ssd_meta_encoder"


def rmsnorm(x, w):
    xf = x.astype(jnp.float32)
    y = xf * lax.rsqrt(jnp.mean(xf * xf, axis=-1, keepdims=True) + NORM_EPS)
    return (y * w.astype(jnp.float32)).astype(x.dtype)


def rope(x, cos, sin):
    xf = x.astype(jnp.float32)
    half = xf.shape[-1] // 2
    x1, x2 = xf[..., :half], xf[..., half:]
    return jnp.concatenate([x1 * cos - x2 * sin, x2 * cos + x1 * sin], axis=-1).astype(x.dtype)


def mla_branch(q_lat, kv_lat, k_rope, cos, sin, q_norm_w, w_uq, kv_norm_w, w_ukv):
    Bz, L, _ = q_lat.shape
    q = (rmsnorm(q_lat, q_norm_w) @ w_uq).reshape(Bz, L, N_ATTN_HEADS, QK_HEAD_DIM)
    kv = (rmsnorm(kv_lat, kv_norm_w) @ w_ukv).reshape(Bz, L, N_ATTN_HEADS, QK_NOPE_DIM + V_HEAD_DIM)
    q_nope, q_pe = q[..., :QK_NOPE_DIM], q[..., QK_NOPE_DIM:]
    k_nope, v = kv[..., :QK_NOPE_DIM], kv[..., QK_NOPE_DIM:]
    q_pe = rope(q_pe, cos[:, None, :], sin[:, None, :])
    k_pe = rope(k_rope, cos, sin)
    k_pe = jnp.broadcast_to(k_pe[:, :, None, :], (Bz, L, N_ATTN_HEADS, QK_ROPE_DIM))
    q = jnp.concatenate([q_nope, q_pe], axis=-1)
    k = jnp.concatenate([k_nope, k_pe], axis=-1)
    n_blk = -(-L // Q_BLOCK)
    q_pad = n_blk * Q_BLOCK - L
    qb = jnp.pad(q, ((0, 0), (0, q_pad), (0, 0), (0, 0)))
    qb = jnp.swapaxes(qb.reshape(Bz, n_blk, Q_BLOCK, N_ATTN_HEADS, QK_HEAD_DIM), 0, 1)
    scale = QK_HEAD_DIM ** -0.5

    def attend(q_blk):
        s = jnp.einsum('bqhd,bkhd->bhqk', q_blk, k).astype(jnp.float32) * scale
        p = jax.nn.softmax(s, axis=-1)
        return jnp.einsum('bhqk,bkhd->bqhd', p.astype(v.dtype), v)

    o = lax.map(attend, qb)
    o = jnp.swapaxes(o, 0, 1).reshape(Bz, n_blk * Q_BLOCK, ATTN_WIDTH)[:, :L]
    return o


def ssd_chunked(x, dt, a, b, c):
    x = x.astype(jnp.float32); dt = dt.astype(jnp.float32)
    b = b.astype(jnp.float32); c = c.astype(jnp.float32)
    Bz, T, H, P = x.shape
    G, N = b.shape[2], b.shape[3]
    J = H // G
    nc = T // CHUNK
    dt_c = dt.reshape(Bz, nc, CHUNK, G, J)
    xdt = x.reshape(Bz, nc, CHUNK, G, J, P) * dt_c[..., None]
    bc = b.reshape(Bz, nc, CHUNK, G, N)
    cc = c.reshape(Bz, nc, CHUNK, G, N)
    dA = jnp.moveaxis(dt_c * a.reshape(G, J), 2, -1)
    acs = jnp.cumsum(dA, axis=-1)
    mask = jnp.tril(jnp.ones((CHUNK, CHUNK), dtype=bool))
    seg = acs[..., :, None] - acs[..., None, :]
    decay = jnp.exp(jnp.where(mask, seg, -jnp.inf))
    cb = jnp.einsum('bclgn,bcsgn->bcgls', cc, bc)
    m = cb[:, :, :, None] * decay
    y_diag = jnp.einsum('bcgjls,bcsgjp->bclgjp', m, xdt)
    decay_states = jnp.moveaxis(jnp.exp(acs[..., -1:] - acs), -1, 2)
    states = jnp.einsum('bclgn,bclgjp->bcgjpn', bc, xdt * decay_states[..., None])
    chunk_decay = jnp.exp(acs[..., -1])

    def step(h, inp):
        s, d = inp
        return h * d[..., None, None] + s, h

    h0 = jnp.zeros((Bz, G, J, P, N), jnp.float32)
    _, h_in = lax.scan(step, h0, (jnp.moveaxis(states, 1, 0), jnp.moveaxis(chunk_decay, 1, 0)))
    h_in = jnp.moveaxis(h_in, 0, 1)
    state_decay = jnp.moveaxis(jnp.exp(acs), -1, 2)
    y_off = jnp.einsum('bclgn,bcgjpn->bclgjp', cc, h_in) * state_decay[..., None]
    return (y_diag + y_off).reshape(Bz, T, H, P)


def ssd_branch(z, xbc, dt_raw, conv_w, conv_b, dt_bias, a_log, d_skip, ssd_norm_w):
    Bz, L, _ = xbc.shape
    xbc = lax.conv_general_dilated(xbc, conv_w[:, None, :], window_strides=(1,),
                                   padding=((CONV_WIDTH // 2, CONV_WIDTH // 2),),
                                   dimension_numbers=('NWC', 'WIO', 'NWC'),
                                   feature_group_count=CONV_DIM)
    xbc = jax.nn.silu(xbc + conv_b)
    gn = N_SSD_GROUPS * D_STATE
    xs, bs, cs = xbc[..., :SSD_WIDTH], xbc[..., SSD_WIDTH:SSD_WIDTH + gn], xbc[..., SSD_WIDTH + gn:]
    dt = jax.nn.softplus(dt_raw.astype(jnp.float32).reshape(Bz, L, 2, N_SSD_HEADS) + dt_bias.astype(jnp.float32))
    a = -jnp.exp(a_log.astype(jnp.float32))
    pad = (-N_META) % CHUNK
    fpad = lambda t: jnp.pad(t, ((0, 0), (pad, 0)) + ((0, 0),) * (t.ndim - 2))
    T = L + pad
    xs_p = fpad(xs).reshape(Bz, T, N_SSD_HEADS, SSD_HEAD_DIM)
    b_p = fpad(bs).reshape(Bz, T, N_SSD_GROUPS, D_STATE)
    c_p = fpad(cs).reshape(Bz, T, N_SSD_GROUPS, D_STATE)
    dt_p = fpad(dt)
    flip = lambda t: jnp.flip(t, axis=1)
    y_f = ssd_chunked(xs_p, dt_p[:, :, 0], a[0], b_p, c_p)
    y_b = flip(ssd_chunked(flip(xs_p), flip(dt_p[:, :, 1]), a[1], flip(b_p), flip(c_p)))
    x4 = xs.reshape(Bz, L, N_SSD_HEADS, SSD_HEAD_DIM).astype(jnp.float32)
    y = (y_f + y_b)[:, pad:] + d_skip.astype(jnp.float32)[:, None] * x4
    y = y.reshape(Bz, L, SSD_WIDTH) * jax.nn.silu(z.astype(jnp.float32))
    y = rmsnorm(y.reshape(Bz, L, N_SSD_GROUPS, SSD_WIDTH // N_SSD_GROUPS),
                ssd_norm_w.reshape(N_SSD_GROUPS, SSD_WIDTH // N_SSD_GROUPS))
    return y.reshape(Bz, L, SSD_WIDTH).astype(xs.dtype)


def setup_inputs(seed: int = 0) -> dict:
    key = jax.random.key(seed)
    ks = jax.random.split(key, 20)
    f32 = jnp.float32
    nrm = lambda k, shape, s: jax.random.normal(k, shape, f32) * s
    gain = lambda k, shape: 1.0 + 0.02 * jax.random.normal(k, shape, f32)
    u = jax.random.uniform(ks[10], (DEPTH, 2, N_SSD_HEADS), f32)
    dt0 = jnp.exp(u * (math.log(0.1) - math.log(0.001)) + math.log(0.001))
    dt_bias = dt0 + jnp.log(-jnp.expm1(-dt0))
    a_log = jnp.log(jax.random.uniform(ks[11], (DEPTH, 2, N_SSD_HEADS), f32, 1.0, 16.0))
    return {
        "x": jax.random.normal(ks[0], (BATCH, SEQ, D_MODEL), f32),
        "meta_tokens": nrm(ks[1], (N_META, D_MODEL), 1.0),
        "norm_w": gain(ks[2], (DEPTH, D_MODEL)),
        "w_in": nrm(ks[3], (DEPTH, D_MODEL, IN_DIM), D_MODEL ** -0.5),
        "q_norm_w": gain(ks[4], (DEPTH, Q_LORA_RANK)),
        "w_uq": nrm(ks[5], (DEPTH, Q_LORA_RANK, N_ATTN_HEADS * QK_HEAD_DIM), Q_LORA_RANK ** -0.5),
        "kv_norm_w": gain(ks[6], (DEPTH, KV_LORA_RANK)),
        "w_ukv": nrm(ks[7], (DEPTH, KV_LORA_RANK, N_ATTN_HEADS * (QK_NOPE_DIM + V_HEAD_DIM)), KV_LORA_RANK ** -0.5),
        "w_attn_proj": nrm(ks[8], (DEPTH, ATTN_WIDTH, D_MODEL), ATTN_WIDTH ** -0.5),
        "conv_w": nrm(ks[9], (DEPTH, CONV_WIDTH, CONV_DIM), CONV_WIDTH ** -0.5),
        "conv_b": nrm(ks[12], (DEPTH, CONV_DIM), 0.02),
        "dt_bias": dt_bias,
        "a_log": a_log,
        "d_skip": 1.0 + 0.1 * jax.random.normal(ks[13], (DEPTH, N_SSD_HEADS), f32),
        "ssd_norm_w": gain(ks[14], (DEPTH, SSD_WIDTH)),
        "w_ssd_proj": nrm(ks[15], (DEPTH, SSD_WIDTH, D_MODEL), SSD_WIDTH ** -0.5),
        "w_out": nrm(ks[16], (DEPTH, D_MODEL, D_MODEL), D_MODEL ** -0.5),
        "final_norm_w": gain(ks[17], (D_MODEL,)),
    }


def reference(x, meta_tokens, norm_w, w_in, q_norm_w, w_uq, kv_norm_w, w_ukv, w_attn_proj,
              conv_w, conv_b, dt_bias, a_log, d_skip, ssd_norm_w, w_ssd_proj, w_out, final_norm_w):
    Bz = x.shape[0]
    meta = jnp.broadcast_to(meta_tokens[None].astype(x.dtype), (Bz, N_META, D_MODEL))
    h = jnp.concatenate([meta, x], axis=1)
    L = h.shape[1]
    pos = jnp.arange(L, dtype=jnp.float32)
    inv_freq = ROPE_THETA ** (-jnp.arange(0, QK_ROPE_DIM, 2, dtype=jnp.float32) / QK_ROPE_DIM)
    ang = pos[:, None] * inv_freq[None, :]
    cos, sin = jnp.cos(ang), jnp.sin(ang)
    offsets = [int(o) for o in np.cumsum(IN_SPLITS)[:-1]]
    for l in range(DEPTH):
        u = rmsnorm(h, norm_w[l])
        proj = u @ w_in[l]
        q_lat, kv_lat, k_rope, g_attn, z, xbc, dt_raw, merge = jnp.split(proj, offsets, axis=-1)
        y_a = mla_branch(q_lat, kv_lat, k_rope, cos, sin, q_norm_w[l], w_uq[l], kv_norm_w[l], w_ukv[l])
        y_a = (y_a * jax.nn.silu(g_attn)) @ w_attn_proj[l]
        y_s = ssd_branch(z, xbc, dt_raw, conv_w[l], conv_b[l], dt_bias[l], a_log[l], d_skip[l], ssd_norm_w[l])
        y_s = y_s @ w_ssd_proj[l]
        gates = jax.nn.sigmoid(merge).reshape(Bz, L, N_BRANCHES, D_MODEL)
        mixed = gates[:, :, 0] * y_a + gates[:, :, 1] * y_s
        h = h + mixed @ w_out[l]
    h = rmsnorm(h, final_norm_w)
    return h[:, N_META:]
```

```python
import contextlib
import numpy as np
import ml_dtypes
import concourse.bass as bass
import concourse.mybir as mybir
from concourse.bass_utils import run_bass_kernel_spmd

F32 = mybir.dt.float32
BF16 = mybir.dt.bfloat16
AF = mybir.ActivationFunctionType
ALU = mybir.AluOpType

D = 1024
KC = 8
NMETA = 16
EPS = 1e-6
NH = 16
SCALE = 96 ** -0.5
import os
STQ = os.environ.get("STQ", "sp")
PL = os.environ.get("PL", "dve")
C_QL, C_KV, C_KR, C_G, C_Z, C_X, C_B, C_C, C_DT, C_MG, C_END = 0, 384, 640, 672, 1696, 3744, 5792, 6304, 6816, 6880, 8928


class Sched:
    ENGS = ("pe", "act", "dve", "pool", "sp")
    NS = 8

    def __init__(self, nc, csem, dsem):
        self.nc = nc
        self.csem = csem
        self.dsem = dsem
        self.ops = {e: [] for e in self.ENGS}
        self.flushed = {e: 0 for e in self.ENGS}
        self.last_w = {}
        self.readers = {}
        self.ndma = {q: 0 for q in ("sp", "pool", "act")}
        self.ccount = {e: 0 for e in self.ENGS}
        self.waited_c = {e: {} for e in self.ENGS}
        self.waited_d = {e: {} for e in self.ENGS}

    def _record(self, eng, method, kwargs, reads, writes, dma):
        self.nrec = getattr(self, "nrec", 0) + 1
        if self.nrec > int(os.environ.get("MAXOPS", "1000000000")):
            return None
        raw = set()
        other = set()
        for r in reads:
            w = self.last_w.get(r)
            if w is not None:
                raw.add(w)
        for wr in writes:
            w = self.last_w.get(wr)
            if w is not None:
                other.add(w)
            rd = self.readers.get(wr)
            if rd:
                for e2, i2 in rd["c"].items():
                    other.add((e2, i2))
                for d2 in rd["d"]:
                    other.add(d2)
        idx = len(self.ops[eng])
        me = (eng, idx)
        deps = []
        for d in raw | other:
            de, di = d
            if d == me:
                continue
            dop = self.ops[de][di]
            if not dop["dma"] and de == eng and eng == "pe":
                continue
            if not dop["dma"]:
                dop["inc"] = True
            deps.append(d)
        rec = dict(method=method, kwargs=kwargs, deps=deps, inc=False, dma=dma, slot=None, val=None, cnt=None)
        if dma:
            k = self.ndma[eng]
            self.ndma[eng] = k + 1
            rec["slot"] = k % self.NS
            rec["val"] = 16 * (k // self.NS + 1)
        self.ops[eng].append(rec)
        for wr in writes:
            self.last_w[wr] = me
            self.readers[wr] = {"c": {}, "d": []}
        for r in reads:
            rd = self.readers.setdefault(r, {"c": {}, "d": []})
            if dma:
                rd["d"].append(me)
            else:
                rd["c"][eng] = idx
        return me

    def op(self, eng, method, reads=(), writes=(), **kwargs):
        return self._record(eng, method, kwargs, reads, writes, False)

    def dma(self, q, out, in_, reads=(), writes=(), **kw):
        kwargs = dict(out=out, in_=in_)
        kwargs.update(kw)
        return self._record(q, "dma_start", kwargs, reads, writes, True)

    def _emit_engine(self, eng, e, upto):
        ops = self.ops[eng]
        for idx in range(self.flushed[eng], upto):
            rec = ops[idx]
            for (de, di) in rec["deps"]:
                dop = self.ops[de][di]
                if dop["dma"]:
                    key = (de, dop["slot"])
                    if self.waited_d[eng].get(key, 0) >= dop["val"]:
                        continue
                    self.waited_d[eng][key] = dop["val"]
                    e.wait_ge(self.dsem[de][dop["slot"]], dop["val"])
                else:
                    if self.waited_c[eng].get(de, -1) >= di:
                        continue
                    self.waited_c[eng][de] = di
                    e.wait_ge(self.csem[de], dop["cnt"])
            if rec["dma"]:
                if rec["val"] > 16:
                    key = (eng, rec["slot"])
                    prev = rec["val"] - 16
                    if self.waited_d[eng].get(key, 0) < prev:
                        self.waited_d[eng][key] = prev
                        e.wait_ge(self.dsem[eng][rec["slot"]], prev)
                ins = getattr(e, rec["method"])(**rec["kwargs"])
                ins.then_inc(self.dsem[eng][rec["slot"]], 16)
            else:
                ins = getattr(e, rec["method"])(**rec["kwargs"])
                if rec["inc"]:
                    ins.then_inc(self.csem[eng], 1)

    def flush(self):
        nc = self.nc
        upto = {e: len(self.ops[e]) for e in self.ENGS}
        for e in ("pe", "act", "dve", "pool"):
            for idx in range(upto[e] - 1, self.flushed[e] - 1, -1):
                if not self.ops[e][idx]["dma"]:
                    self.ops[e][idx]["inc"] = True
                    break
        for e in self.ENGS:
            c = self.ccount[e]
            for idx in range(self.flushed[e], upto[e]):
                rec = self.ops[e][idx]
                if not rec["dma"] and rec["inc"]:
                    c += 1
                rec["cnt"] = c
            self.ccount[e] = c
        final_c = dict(self.ccount)
        final_d = {}
        for q in ("sp", "pool", "act"):
            k = self.ndma[q]
            for slot in range(self.NS):
                n_uses = (k - slot + self.NS - 1) // self.NS if k > slot else 0
                if n_uses > 0:
                    final_d[(q, slot)] = 16 * n_uses
        sched = self

        def body(eng):
            def f(e):
                sched._emit_engine(eng, e, upto[eng])
                for ce in ("pe", "act", "dve", "pool"):
                    if ce != eng and final_c[ce] > 0:
                        e.wait_ge(sched.csem[ce], final_c[ce])
                for (q, slot), v in final_d.items():
                    if sched.waited_d[eng].get((q, slot), 0) < v:
                        sched.waited_d[eng][(q, slot)] = v
                        e.wait_ge(sched.dsem[q][slot], v)
            return f

        with nc.Block() as block:
            block.tensor(body("pe"))
            block.scalar(body("act"))
            block.vector(body("dve"))
            block.gpsimd(body("pool"))
            block.sync(body("sp"))
        for e in self.ENGS:
            self.flushed[e] = upto[e]
            for ce in ("pe", "act", "dve", "pool"):
                if ce != e and upto[ce] > 0:
                    self.waited_c[e][ce] = upto[ce] - 1
        self.last_w = {}
        self.readers = {}


def tok0(i):
    return 0 if i == 0 else NMETA + 128 * (i - 1)


def ntok(i):
    return NMETA if i == 0 else 128


def build_nc(NT, phases="ABS012F", dbg=()):
    L = NMETA + 128 * NT
    NTT = NT + 1
    SEQ = 128 * NT
    nc = bass.Bass("TRN2", target_bir_lowering=False)

    def din(name, shape, dt=F32):
        return nc.dram_tensor(name, list(shape), dt, kind="ExternalInput").ap()

    x_d = din("x", [SEQ, D])
    meta_d = din("meta", [NMETA, D])
    normw_d = din("norm_w", [1, D])
    win_d = din("w_in", [D, C_END])
    gains_d = din("lat_gains", [1, 640])
    wuq_d = din("w_uq", [384, 1536])
    wukv_d = din("w_ukv", [256, 2048])
    wap_d = din("w_attn_proj", [D, D])
    convw_d = din("convw_fm", [128, 24, 5])
    convb_d = din("convb_fm", [128, 24])
    dtb_d = din("dt_bias", [1, 64])
    alog_d = din("a_log", [1, 64])
    dskip_d = din("d_skip", [1, 32])
    ssdnw_d = din("ssd_norm_w", [1, 2048])
    wsp_d = din("w_ssd_proj", [2048, D])
    wout_d = din("w_out", [D, D])
    fnw_d = din("final_norm_w", [1, D])
    ident_d = din("ident", [128, 128])
    cstok_d = din("cs_tok", [L, 32])
    cos2_d = din("cos2", [32, L])
    sin2_d = din("sin2", [32, L])
    masks_d = din("masks", [4, 128, 128])
    out_d = nc.dram_tensor("out", [SEQ, D], F32, kind="ExternalOutput").ap()

    def dscr(name, shape, dt):
        kind = "ExternalOutput" if name in dbg else "Internal"
        return nc.dram_tensor(name, list(shape), dt, kind=kind).ap()

    uT_d = dscr("uT_d", [128, KC, L], BF16)
    og_d = dscr("og_d", [L, D], BF16)
    xs_d = dscr("xs_d", [L, 2048], F32)
    btok_d = dscr("btok_d", [L, 512], BF16)
    bT_d = dscr("bT_d", [128, 4, L], BF16)
    cT_d = dscr("cT_d", [128, 4, L], BF16)
    dt_d = dscr("dt_d", [L, 64], F32)
    yb_d = dscr("yb_d", [L, 2048], F32)
    yn_d = dscr("yn_d", [L, 2048], BF16)
    dbgA_d = dscr("dbgA_d", [128, 6, L], BF16) if "dbgA_d" in dbg else None

    with contextlib.ExitStack() as es0:
        csem = {e: es0.enter_context(nc.semaphore("c_" + e)) for e in ("pe", "act", "dve", "pool")}
        dsem = {q: [es0.enter_context(nc.semaphore(f"d_{q}{i}")) for i in range(Sched.NS)] for q in ("sp", "pool", "act")}
        S = Sched(nc, csem, dsem)

        def mk(es, pfx=""):
            def sb(name, shape, dt):
                return es.enter_context(nc.sbuf_tensor(name + pfx, list(shape), dt))

            def ps(name, shape, dt):
                return es.enter_context(nc.psum_tensor(name + pfx, list(shape), dt))
            return sb, ps

        sb0, ps0 = mk(es0)
        idf = sb0("idf", [128, 128], F32)
        idb = sb0("idb", [128, 128], BF16)
        S.dma("sp", out=idf[:], in_=ident_d[:, :], writes=["idf"])
        S.op("dve", "tensor_copy", out=idb[:], in_=idf[:], reads=["idf"], writes=["idb"])

        wl_state = {"n": 0}

        def load_weight(stage, dst, dram, kcn, col0, ncols, dst_col0=0):
            for k in range(kcn):
                for c in range(0, ncols, 2048):
                    w = min(2048, ncols - c)
                    s = wl_state["n"] % len(stage)
                    wl_state["n"] += 1
                    S.dma("sp", out=stage[s][:, 0:w], in_=dram[k * 128:(k + 1) * 128, col0 + c:col0 + c + w],
                          writes=[("stage", s)])
                    eng = "dve" if (wl_state["n"] % 2) else "pool"
                    S.op(eng, "tensor_copy", out=dst[:, k, dst_col0 + c:dst_col0 + c + w], in_=stage[s][:, 0:w],
                         reads=[("stage", s)], writes=[dst.name])

        def bcast_load(dst, dram_row, ncols):
            S.dma("sp", out=dst[:, 0:ncols], in_=dram_row[0:1, 0:ncols].partition_broadcast(128), writes=[dst.name])

        if "A" in phases or "B" in phases:
            with contextlib.ExitStack() as esAB:
                sbAB, psAB = mk(esAB)
                cqT = sbAB("cqT", [128, 3, L], BF16)
                ckvT = sbAB("ckvT", [128, 2, L], BF16)
                KT = sbAB("KT", [96, L], BF16)
                with contextlib.ExitStack() as esA:
                    sb, ps = mk(esA)
                    stage = [sb(f"stageA{i}", [128, 2048], F32) for i in range(2)]
                    wlat = sb("wlat", [128, KC, 672], BF16)
                    load_weight(stage, wlat, win_d, KC, 0, 672)
                    normw_bc = sb("normw_bc", [128, D], F32)
                    bcast_load(normw_bc, normw_d, D)
                    gains_bc = sb("gains_bc", [128, 640], F32)
                    bcast_load(gains_bc, gains_d, 640)
                    cs = sb("cs", [128, NTT, 32], F32)
                    S.op("dve", "memset", ap=cs[:], constant=0.0, writes=["cs"])
                    S.dma("sp", out=cs[0:NMETA, 0, :], in_=cstok_d[0:NMETA, :], reads=["cs"], writes=["cs"])
                    for i0 in range(1, NTT, 16):
                        i1 = min(NTT, i0 + 16)
                        S.dma("sp", out=cs[:, i0:i1, :],
                              in_=cstok_d[tok0(i0):tok0(i0) + 128 * (i1 - i0), :].rearrange("(t p) c -> p t c", p=128),
                              reads=["cs"], writes=["cs"])
                    xt = [sb(f"xtA{i}", [128, D], F32) for i in range(2)]
                    junk = sb("junkA", [128, D], F32)
                    ss = [sb(f"ssA{i}", [128, 4], F32) for i in range(2)]
                    u = [sb(f"uA{i}", [128, D], BF16) for i in range(2)]
                    uT = [sb(f"uTA{i}", [128, KC, 128], BF16) for i in range(2)]
                    latn = [sb(f"latn{i}", [128, 736], BF16) for i in range(2)]
                    kr = [sb(f"kr{i}", [128, 32], F32) for i in range(2)]
                    rt = [sb(f"rt{i}", [128, 64], F32) for i in range(2)]
                    dbgA = [sb(f"dbgA{i}", [128, 6, 128], BF16) for i in range(2)] if dbgA_d is not None else None
                    if dbgA is not None:
                        for s_ in range(2):
                            S.op("dve", "memset", ap=dbgA[s_][:], constant=0.0, writes=[("dbgA", s_)])
                    ptr = ps("ptrA", [128, KC, 128], BF16)
                    plq = ps("plq", [128, 512], F32)
                    plkv = ps("plkv", [128, 512], F32)
                    ptl = ps("ptl", [128, 6, 128], BF16)
                    for s in range(2):
                        S.op("dve", "memset", ap=latn[s][:, 640:704], constant=0.0, writes=[("latn", s)])
                    def loadA(i):
                        n, s = ntok(i), i % 2
                        src = meta_d[:, :] if i == 0 else x_d[(i - 1) * 128:i * 128, :]
                        S.dma("sp", out=xt[s][0:n, :], in_=src, writes=[("xt", s)])

                    loadA(0)
                    for i in range(NTT):
                        n, t0, s = ntok(i), tok0(i), i % 2
                        if i + 1 < NTT:
                            loadA(i + 1)
                        S.op("act", "activation", out=junk[0:n, :], in_=xt[s][0:n, :], func=AF.Square,
                             accum_out=ss[s][0:n, 0:1], reads=[("xt", s)], writes=["junk", ("ss", s)])
                        S.op("act", "activation", out=ss[s][0:n, 0:1], in_=ss[s][0:n, 0:1], func=AF.Sqrt,
                             scale=1.0 / D, bias=EPS, reads=[("ss", s)], writes=[("ss", s)])
                        S.op("dve", "reciprocal", out=ss[s][0:n, 0:1], in_=ss[s][0:n, 0:1], reads=[("ss", s)], writes=[("ss", s)])
                        S.op("dve", "scalar_tensor_tensor", out=u[s][0:n, :], in0=xt[s][0:n, :], scalar=ss[s][0:n, 0:1],
                             in1=normw_bc[0:n, :], op0=ALU.mult, op1=ALU.mult,
                             reads=[("xt", s), ("ss", s), "normw_bc"], writes=[("u", s)])
                        for k in range(KC):
                            S.op("pe", "transpose", out=ptr[:, k, 0:n], in_=u[s][0:n, k * 128:(k + 1) * 128],
                                 identity=idb[0:n, 0:n], reads=[("u", s), "idb"], writes=["ptr"])
                        S.op("act", "copy", out=uT[s][:, :, 0:n], in_=ptr[:, :, 0:n], reads=["ptr"], writes=[("uT", s)])
                        S.dma(STQ, out=uT_d[:, :, t0:t0 + n], in_=uT[s][:, :, 0:n], reads=[("uT", s)], writes=[("uT_d", i)])
                        for (c0, c1, bank, bn) in ((0, 384, plq, "plq"), (384, 672, plkv, "plkv")):
                            for k in range(KC):
                                S.op("pe", "matmul", out=bank[0:n, 0:c1 - c0], lhsT=uT[s][:, k, 0:n], rhs=wlat[:, k, c0:c1],
                                     start=(k == 0), stop=(k == KC - 1), reads=[("uT", s), "wlat"], writes=[bn])
                        S.op("act", "activation", out=junk[0:n, 0:384], in_=plq[0:n, 0:384], func=AF.Square,
                             accum_out=ss[s][0:n, 1:2], reads=["plq"], writes=["junk", ("ssq", s)])
                        S.op("act", "activation", out=junk[0:n, 0:256], in_=plkv[0:n, 0:256], func=AF.Square,
                             accum_out=ss[s][0:n, 2:3], reads=["plkv"], writes=["junk", ("ssq", s)])
                        S.op("act", "activation", out=ss[s][0:n, 1:2], in_=ss[s][0:n, 1:2], func=AF.Sqrt,
                             scale=1.0 / 384, bias=EPS, reads=[("ssq", s)], writes=[("ssq", s)])
                        S.op("act", "activation", out=ss[s][0:n, 2:3], in_=ss[s][0:n, 2:3], func=AF.Sqrt,
                             scale=1.0 / 256, bias=EPS, reads=[("ssq", s)], writes=[("ssq", s)])
                        S.op("act", "copy", out=kr[s][0:n, :], in_=plkv[0:n, 256:288], reads=["plkv"], writes=[("kr", s)])
                        S.op("dve", "reciprocal", out=ss[s][0:n, 1:3], in_=ss[s][0:n, 1:3], reads=[("ssq", s)], writes=[("ssq", s)])
                        S.op("dve", "scalar_tensor_tensor", out=latn[s][0:n, 0:384], in0=plq[0:n, 0:384], scalar=ss[s][0:n, 1:2],
                             in1=gains_bc[0:n, 0:384], op0=ALU.mult, op1=ALU.mult,
                             reads=["plq", ("ssq", s), "gains_bc"], writes=[("latn", s)])
                        S.op("dve", "scalar_tensor_tensor", out=latn[s][0:n, 384:640], in0=plkv[0:n, 0:256], scalar=ss[s][0:n, 2:3],
                             in1=gains_bc[0:n, 384:640], op0=ALU.mult, op1=ALU.mult,
                             reads=["plkv", ("ssq", s), "gains_bc"], writes=[("latn", s)])
                        cosv, sinv = cs[0:n, i, 0:16], cs[0:n, i, 16:32]
                        x1, x2 = kr[s][0:n, 0:16], kr[s][0:n, 16:32]
                        S.op(PL, "tensor_tensor", out=rt[s][0:n, 0:16], in0=x1, in1=cosv, op=ALU.mult, reads=[("kr", s), "cs"], writes=[("rt", s, 0)])
                        S.op(PL, "tensor_tensor", out=rt[s][0:n, 16:32], in0=x2, in1=sinv, op=ALU.mult, reads=[("kr", s), "cs"], writes=[("rt", s, 1)])
                        S.op(PL, "tensor_tensor", out=rt[s][0:n, 32:48], in0=x2, in1=cosv, op=ALU.mult, reads=[("kr", s), "cs"], writes=[("rt", s, 2)])
                        S.op(PL, "tensor_tensor", out=rt[s][0:n, 48:64], in0=x1, in1=sinv, op=ALU.mult, reads=[("kr", s), "cs"], writes=[("rt", s, 3)])
                        S.op(PL, "tensor_tensor", out=latn[s][0:n, 704:720], in0=rt[s][0:n, 0:16], in1=rt[s][0:n, 16:32], op=ALU.subtract,
                             reads=[("rt", s, 0), ("rt", s, 1)], writes=[("latn", s)])
                        S.op(PL, "tensor_tensor", out=latn[s][0:n, 720:736], in0=rt[s][0:n, 32:48], in1=rt[s][0:n, 48:64], op=ALU.add,
                             reads=[("rt", s, 2), ("rt", s, 3)], writes=[("latn", s)])
                        for c in range(5):
                            S.op("pe", "transpose", out=ptl[:, c, 0:n], in_=latn[s][0:n, c * 128:(c + 1) * 128],
                                 identity=idb[0:n, 0:n], reads=[("latn", s), "idb"], writes=["ptl"])
                        S.op("pe", "transpose", out=ptl[0:96, 5, 0:n], in_=latn[s][0:n, 640:736],
                             identity=idb[0:n, 0:n], reads=[("latn", s), "idb"], writes=["ptl"])
                        S.op("dve", "tensor_copy", out=cqT[:, :, t0:t0 + n], in_=ptl[:, 0:3, 0:n], reads=["ptl"], writes=["cqT"])
                        S.op("dve", "tensor_copy", out=ckvT[:, :, t0:t0 + n], in_=ptl[:, 3:5, 0:n], reads=["ptl"], writes=["ckvT"])
                        S.op("dve", "tensor_copy", out=KT[64:96, t0:t0 + n], in_=ptl[64:96, 5, 0:n], reads=["ptl"], writes=["KTpe"])
                        if dbgA is not None:
                            S.op("dve", "tensor_copy", out=dbgA[s][:, 0:5, 0:n], in_=ptl[:, 0:5, 0:n], reads=["ptl"], writes=[("dbgA", s)])
                            S.op("dve", "tensor_copy", out=dbgA[s][0:96, 5, 0:n], in_=ptl[0:96, 5, 0:n], reads=["ptl"], writes=[("dbgA", s)])
                            S.dma(STQ, out=dbgA_d[:, :, t0:t0 + n], in_=dbgA[s][:, :, 0:n], reads=[("dbgA", s)], writes=[("dbgA_d", i)])
                    S.flush()
                if "B" in phases:
                    with contextlib.ExitStack() as esB:
                        sb, ps = mk(esB)
                        NKT = (L + 127) // 128
                        NQG = (L + 511) // 512
                        stage = [sb(f"stageB{i}", [128, 2048], F32) for i in range(2)]
                        wuq = sb("wuq", [128, 3, 1536], BF16)
                        wsw = sb("wsw", [128, 3, 1536], BF16)
                        wukv = sb("wukv", [128, 2, 2048], BF16)
                        load_weight(stage, wuq, wuq_d, 3, 0, 1536)
                        load_weight(stage, wukv, wukv_d, 2, 0, 2048)
                        wuq4 = wuq[:].rearrange("p k (h c) -> p (k h) c", c=96)
                        wsw4 = wsw[:].rearrange("p k (h c) -> p (k h) c", c=96)
                        S.op("dve", "tensor_copy", out=wsw[:], in_=wuq[:], reads=["wuq"], writes=["wsw"])
                        S.op("dve", "tensor_scalar", out=wsw4[:, :, 64:80], in0=wuq4[:, :, 80:96], scalar1=-1.0, scalar2=None, op0=ALU.mult,
                             reads=["wuq", "wsw"], writes=["wsw"])
                        S.op("dve", "tensor_copy", out=wsw4[:, :, 80:96], in_=wuq4[:, :, 64:80], reads=["wuq", "wsw"], writes=["wsw"])
                        Vt = sb("Vt", [128, NKT, 65], BF16)
                        S.op("dve", "memset", ap=Vt[:, :, 64:65], constant=1.0, writes=["Vt1"])
                        QT = [sb(f"QT{i}", [96, 512], BF16) for i in range(2)]
                        PT = [sb(f"PT{i}", [128, 512], BF16) for i in range(2)]
                        cos2t = [sb(f"cos2t{i}", [96, 512], F32) for i in range(2)]
                        sin2t = [sb(f"sin2t{i}", [96, 512], F32) for i in range(2)]
                        tq1 = [sb(f"tq1{i}", [96, 512], F32) for i in range(2)]
                        tq2 = [sb(f"tq2{i}", [96, 512], F32) for i in range(2)]
                        osb = [sb(f"osb{i}", [128, 4, 64], BF16) for i in range(2)]
                        rinv = [sb(f"rinv{i}", [128, 4], F32) for i in range(2)]
                        Sps = [ps(f"Sps{i}", [128, 512], F32) for i in range(2)]
                        Ops = [ps(f"Ops{i}", [128, 512], F32) for i in range(4)]
                        pq = ps("pq", [128, 512], F32)
                        pq2 = pq
                        pkv = ps("pkv", [128, 512], F32)
                        qcount = 0
                        scount = 0

                        def loadB(idx):
                            g_ = idx % NQG
                            a_ = g_ * 512
                            nq_ = min(512, L - a_)
                            q_ = idx % 2
                            S.dma("sp", out=cos2t[q_][64:96, 0:nq_], in_=cos2_d[:, a_:a_ + nq_], writes=[("cos2t", q_)])
                            S.dma("sp", out=sin2t[q_][64:96, 0:nq_], in_=sin2_d[:, a_:a_ + nq_], writes=[("sin2t", q_)])

                        loadB(0)
                        for h in range(NH):
                            for tg in range(NQG):
                                a = tg * 512
                                nt = min(512, L - a)
                                for kc in range(2):
                                    S.op("pe", "matmul", out=pkv[0:64, 0:nt], lhsT=wukv[:, kc, h * 128:h * 128 + 64], rhs=ckvT[:, kc, a:a + nt],
                                         start=(kc == 0), stop=(kc == 1), reads=["wukv"], writes=["pkv"])
                                S.op("dve", "tensor_copy", out=KT[0:64, a:a + nt], in_=pkv[0:64, 0:nt], reads=["pkv"], writes=["KTn"])
                            for j0 in range(0, NKT, 8):
                                j1 = min(NKT, j0 + 8)
                                for j in range(j0, j1):
                                    nk = min(128, L - 128 * j)
                                    for kc in range(2):
                                        S.op("pe", "matmul", out=pkv[0:nk, (j - j0) * 64:(j - j0 + 1) * 64],
                                             lhsT=ckvT[:, kc, 128 * j:128 * j + nk], rhs=wukv[:, kc, h * 128 + 64:h * 128 + 128],
                                             start=(kc == 0), stop=(kc == 1), reads=["wukv"], writes=["pkv"])
                                nfull = sum(1 for j in range(j0, j1) if L - 128 * j >= 128)
                                if nfull > 0:
                                    S.op("dve", "tensor_copy", out=Vt[:, j0:j0 + nfull, 0:64],
                                         in_=pkv[:, 0:nfull * 64].rearrange("p (j c) -> p j c", c=64),
                                         reads=["pkv"], writes=["Vt"])
                                if nfull < j1 - j0:
                                    jl = j1 - 1
                                    nkl = L - 128 * jl
                                    S.op("dve", "tensor_copy", out=Vt[0:nkl, jl, 0:64], in_=pkv[0:nkl, (jl - j0) * 64:(jl - j0 + 1) * 64],
                                         reads=["pkv"], writes=["Vt"])
                            for g in range(NQG):
                                a = g * 512
                                nq = min(512, L - a)
                                nsub = (nq + 127) // 128
                                qs = qcount % 2
                                qcount += 1
                                for kc in range(3):
                                    S.op("pe", "matmul", out=pq[0:96, 0:nq], lhsT=wuq[:, kc, h * 96:(h + 1) * 96], rhs=cqT[:, kc, a:a + nq],
                                         start=(kc == 0), stop=(kc == 2), reads=["wuq"], writes=["pq"])
                                S.op("dve", "tensor_copy", out=QT[qs][0:64, 0:nq], in_=pq[0:64, 0:nq], reads=["pq"], writes=[("QTn", qs)])
                                S.op("dve", "tensor_tensor", out=tq1[qs][64:96, 0:nq], in0=pq[64:96, 0:nq], in1=cos2t[qs][64:96, 0:nq], op=ALU.mult,
                                     reads=["pq", ("cos2t", qs)], writes=[("tq1", qs)])
                                for kc in range(3):
                                    S.op("pe", "matmul", out=pq2[0:96, 0:nq], lhsT=wsw[:, kc, h * 96:(h + 1) * 96], rhs=cqT[:, kc, a:a + nq],
                                         start=(kc == 0), stop=(kc == 2), reads=["wsw"], writes=["pq"])
                                S.op("dve", "tensor_tensor", out=tq2[qs][64:96, 0:nq], in0=pq2[64:96, 0:nq], in1=sin2t[qs][64:96, 0:nq], op=ALU.mult,
                                     reads=["pq", ("sin2t", qs)], writes=[("tq2", qs)])
                                S.op(PL, "tensor_tensor", out=QT[qs][64:96, 0:nq], in0=tq1[qs][64:96, 0:nq], in1=tq2[qs][64:96, 0:nq], op=ALU.add,
                                     reads=[("tq1", qs), ("tq2", qs)], writes=[("QTp", qs)])
                                def emitS(j):
                                    nonlocal scount
                                    sbk = scount % 2
                                    scount += 1
                                    nk = min(128, L - 128 * j)
                                    S.op("pe", "matmul", out=Sps[sbk][0:nk, 0:nq], lhsT=KT[0:96, 128 * j:128 * j + nk], rhs=QT[qs][0:96, 0:nq],
                                         start=True, stop=True, reads=["KTn", ("QTn", qs), ("QTp", qs)], writes=[("S", sbk)])
                                    S.op("act", "activation", out=PT[sbk][0:nk, 0:nq], in_=Sps[sbk][0:nk, 0:nq], func=AF.Exp, scale=SCALE,
                                         reads=[("S", sbk)], writes=[("PT", sbk)])
                                    return sbk

                                def emitPV(j, sbk):
                                    nk = min(128, L - 128 * j)
                                    for sub in range(nsub):
                                        nqs = min(128, nq - sub * 128)
                                        S.op("pe", "matmul", out=Ops[sub][0:nqs, 0:65],
                                             lhsT=PT[sbk][0:nk, sub * 128:sub * 128 + nqs], rhs=Vt[0:nk, j, 0:65],
                                             start=(j == 0), stop=(j == NKT - 1),
                                             reads=[("PT", sbk), "Vt", "Vt1"], writes=[("O", sub)])

                                if qcount < NH * NQG:
                                    loadB(qcount)
                                pend = []
                                for j in range(NKT):
                                    pend.append((j, emitS(j)))
                                    if len(pend) > 1:
                                        emitPV(*pend.pop(0))
                                while pend:
                                    emitPV(*pend.pop(0))
                                for sub in range(nsub):
                                    nqs = min(128, nq - sub * 128)
                                    S.op("dve", "reciprocal", out=rinv[qs][0:nqs, sub:sub + 1], in_=Ops[sub][0:nqs, 64:65],
                                         reads=[("O", sub)], writes=[("rinv", qs, sub)])
                                    S.op("dve", "tensor_scalar", out=osb[qs][0:nqs, sub, :], in0=Ops[sub][0:nqs, 0:64],
                                         scalar1=rinv[qs][0:nqs, sub:sub + 1], scalar2=None, op0=ALU.mult,
                                         reads=[("O", sub), ("rinv", qs, sub)], writes=[("osb", qs)])
                                if nq == 512:
                                    S.dma(STQ, out=og_d[a:a + 512, h * 64:(h + 1) * 64].rearrange("(s p) c -> p s c", p=128),
                                          in_=osb[qs][:, :, :], reads=[("osb", qs)], writes=[("og_d", h, g)])
                                else:
                                    for sub in range(nsub):
                                        nqs = min(128, nq - sub * 128)
                                        S.dma(STQ, out=og_d[a + sub * 128:a + sub * 128 + nqs, h * 64:(h + 1) * 64],
                                              in_=osb[qs][0:nqs, sub, :], reads=[("osb", qs)], writes=[("og_d", h, g, sub)])
                        S.flush()

        groups = [[0]] + [list(range(i, min(i + 2, NTT))) for i in range(1, NTT, 2)]
        if "0" in phases:
            with contextlib.ExitStack() as esS0:
                sb, ps = mk(esS0)
                stage = [sb(f"stageS{i}", [128, 2048], F32) for i in range(2)]
                wxbc = sb("wxbc", [128, KC, 3072], BF16)
                wdt = sb("wdt", [128, KC, 64], BF16)
                load_weight(stage, wxbc, win_d, KC, C_X, 3072)
                load_weight(stage, wdt, win_d, KC, C_DT, 64)
                convw = sb("convw", [128, 24, 5], F32)
                convb = sb("convb", [128, 24], F32)
                S.dma("sp", out=convw[:], in_=convw_d[:, :, :], writes=["convw"])
                S.dma("sp", out=convb[:], in_=convb_d[:, :], writes=["convb"])
                dtb_bc = sb("dtb_bc", [128, 64], F32)
                bcast_load(dtb_bc, dtb_d, 64)
                uTw = [sb(f"uTw{i}", [128, KC, 260], BF16) for i in range(2)]
                acc = [sb(f"acc{i}", [128, 256], F32) for i in range(3)]
                xsT = sb("xsT", [128, 16, 256], F32)
                BTg = [sb(f"BTg{i}", [128, 4, 256], BF16) for i in range(2)]
                CTg = [sb(f"CTg{i}", [128, 4, 256], BF16) for i in range(2)]
                xs_tok = [sb(f"xs_tok{i}", [128, 2048], F32) for i in range(2)]
                B_tok = [sb(f"B_tok{i}", [128, 512], BF16) for i in range(2)]
                dtx = [sb(f"dtx{i}", [128, 64], F32) for i in range(2)]
                dta = [sb(f"dta{i}", [128, 64], F32) for i in range(2)]
                dtt = [sb(f"dtt{i}", [128, 64], F32) for i in range(2)]
                praw = [ps(f"praw{i}", [128, 512], F32) for i in range(3)]
                ptx = [ps(f"ptx{i}", [128, 512], F32) for i in range(2)]
                ptb = ps("ptb", [128, 512], BF16)
                pdt = ps("pdt", [128, 512], F32)
                tcount = 0
                def loadS0(gi):
                    tiles = groups[gi]
                    a = tok0(tiles[0])
                    n = sum(ntok(t) for t in tiles)
                    s = gi % 2
                    lo, hi = max(0, a - 2), min(L, a + n + 2)
                    c0 = lo - (a - 2)
                    c1 = c0 + (hi - lo)
                    wr = [("uTw", s)]
                    if c0 > 0:
                        S.op("dve", "memset", ap=uTw[s][:, :, 0:c0], constant=0.0, writes=wr)
                    if c1 < n + 4:
                        S.op("dve", "memset", ap=uTw[s][:, :, c1:n + 4], constant=0.0, writes=wr)
                    S.dma("sp", out=uTw[s][:, :, c0:c1], in_=uT_d[:, :, lo:hi], reads=[("uT_d", "all")], writes=wr)

                loadS0(0)
                for gi, tiles in enumerate(groups):
                    a = tok0(tiles[0])
                    n = sum(ntok(t) for t in tiles)
                    s = gi % 2
                    if gi + 1 < len(groups):
                        loadS0(gi + 1)
                    for cc in range(24):
                        rb = cc % 3
                        for k in range(KC):
                            S.op("pe", "matmul", out=praw[rb][:, 0:n + 4], lhsT=wxbc[:, k, cc * 128:(cc + 1) * 128], rhs=uTw[s][:, k, 0:n + 4],
                                 start=(k == 0), stop=(k == KC - 1), reads=[("uTw", s), "wxbc"], writes=[("praw", rb)])
                        S.op("dve", "tensor_scalar", out=acc[rb][:, 0:n], in0=praw[rb][:, 0:n], scalar1=convw[:, cc, 0:1], scalar2=None, op0=ALU.mult,
                             reads=[("praw", rb), "convw"], writes=[("acc", rb)])
                        for j in range(1, 5):
                            S.op("dve", "scalar_tensor_tensor", out=acc[rb][:, 0:n], in0=praw[rb][:, j:j + n], scalar=convw[:, cc, j:j + 1],
                                 in1=acc[rb][:, 0:n], op0=ALU.mult, op1=ALU.add, reads=[("praw", rb), "convw", ("acc", rb)], writes=[("acc", rb)])
                        if cc < 16:
                            dst, wname = xsT[:, cc, 0:n], ("xsT", cc)
                        elif cc < 20:
                            dst, wname = BTg[s][:, cc - 16, 0:n], ("BTg", s)
                        else:
                            dst, wname = CTg[s][:, cc - 20, 0:n], ("CTg", s)
                        S.op("act", "activation", out=dst, in_=acc[rb][:, 0:n], func=AF.Silu, bias=convb[:, cc:cc + 1],
                             reads=[("acc", rb), "convb"], writes=[wname])
                    S.dma(STQ, out=bT_d[:, :, a:a + n], in_=BTg[s][:, :, 0:n], reads=[("BTg", s)], writes=[("bT_d", gi)])
                    S.dma(STQ, out=cT_d[:, :, a:a + n], in_=CTg[s][:, :, 0:n], reads=[("CTg", s)], writes=[("cT_d", gi)])
                    o = 0
                    for t in tiles:
                        nt, t0 = ntok(t), tok0(t)
                        ts_ = tcount % 2
                        tcount += 1
                        for q in range(4):
                            pb_ = q % 2
                            for c4 in range(4):
                                cc = q * 4 + c4
                                S.op("pe", "transpose", out=ptx[pb_][0:nt, c4 * 128:(c4 + 1) * 128], in_=xsT[:, cc, o:o + nt], identity=idf[:],
                                     reads=[("xsT", cc), "idf"], writes=[("ptx", pb_)])
                            S.op("act" if q % 2 else "dve", "copy" if q % 2 else "tensor_copy", out=xs_tok[ts_][0:nt, q * 512:(q + 1) * 512], in_=ptx[pb_][0:nt, :],
                                 reads=[("ptx", pb_)], writes=[("xs_tok", ts_)])
                        for g4 in range(4):
                            S.op("pe", "transpose", out=ptb[0:nt, g4 * 128:(g4 + 1) * 128], in_=BTg[s][:, g4, o:o + nt], identity=idb[:],
                                 reads=[("BTg", s), "idb"], writes=["ptb"])
                        S.op("dve", "tensor_copy", out=B_tok[ts_][0:nt, :], in_=ptb[0:nt, :], reads=["ptb"], writes=[("B_tok", ts_)])
                        S.dma(STQ, out=xs_d[t0:t0 + nt, :], in_=xs_tok[ts_][0:nt, :], reads=[("xs_tok", ts_)], writes=[("xs_d", t)])
                        S.dma(STQ, out=btok_d[t0:t0 + nt, :], in_=B_tok[ts_][0:nt, :], reads=[("B_tok", ts_)], writes=[("btok_d", t)])
                        for k in range(KC):
                            S.op("pe", "matmul", out=pdt[0:nt, 0:64], lhsT=uTw[s][:, k, 2 + o:2 + o + nt], rhs=wdt[:, k, 0:64],
                                 start=(k == 0), stop=(k == KC - 1), reads=[("uTw", s), "wdt"], writes=["pdt"])
                        S.op("dve", "tensor_tensor", out=dtx[ts_][0:nt, :], in0=pdt[0:nt, 0:64], in1=dtb_bc[0:nt, :], op=ALU.add,
                             reads=["pdt", "dtb_bc"], writes=[("dtx", ts_)])
                        S.op("act", "activation", out=dta[ts_][0:nt, :], in_=dtx[ts_][0:nt, :], func=AF.Abs,
                             reads=[("dtx", ts_)], writes=[("dta", ts_)])
                        S.op("act", "activation", out=dta[ts_][0:nt, :], in_=dta[ts_][0:nt, :], func=AF.Exp, scale=-1.0,
                             reads=[("dta", ts_)], writes=[("dta", ts_)])
                        S.op("act", "activation", out=dta[ts_][0:nt, :], in_=dta[ts_][0:nt, :], func=AF.Ln, bias=1.0,
                             reads=[("dta", ts_)], writes=[("dta", ts_)])
                        S.op("dve", "tensor_scalar_max", out=dtx[ts_][0:nt, :], in0=dtx[ts_][0:nt, :], scalar1=0.0,
                             reads=[("dtx", ts_)], writes=[("dtx", ts_)])
                        S.op("dve", "tensor_tensor", out=dtt[ts_][0:nt, :], in0=dtx[ts_][0:nt, :], in1=dta[ts_][0:nt, :], op=ALU.add,
                             reads=[("dtx", ts_), ("dta", ts_)], writes=[("dtt", ts_)])
                        S.dma(STQ, out=dt_d[t0:t0 + nt, :], in_=dtt[ts_][0:nt, :], reads=[("dtt", ts_)], writes=[("dt_d", t)])
                        o += nt
                S.flush()

        def ssd_pass(d, final):
            with contextlib.ExitStack() as esS:
                sb, ps = mk(esS, f"_p{d}")
                masks = sb("masks", [128, 4, 128], F32)
                S.dma("sp", out=masks[:], in_=masks_d.rearrange("m r c -> r m c"), writes=["masks"])
                mask_d = masks[:, d, :]
                L_d = masks[:, 2 + d, :]
                ones = sb("ones", [128, 128], F32)
                S.op("dve", "memset", ap=ones[:], constant=1.0, writes=["ones"])
                aneg = sb("aneg", [128, 64], F32)
                bcast_load(aneg, alog_d, 64)
                S.op("act", "activation", out=aneg[:], in_=aneg[:], func=AF.Exp, reads=[aneg.name], writes=[aneg.name])
                S.op("dve", "tensor_scalar", out=aneg[:], in0=aneg[:], scalar1=-1.0, scalar2=None, op0=ALU.mult, reads=[aneg.name], writes=[aneg.name])
                hT = sb("hT", [128, 2048], F32)
                hTb = sb("hTb", [128, 2048], BF16)
                S.op("dve", "memset", ap=hT[:], constant=0.0, writes=[("hT", g) for g in range(4)])
                xs = [sb(f"xs{i}", [128, 2048], F32) for i in range(2)]
                Bt = [sb(f"Bt{i}", [128, 512], BF16) for i in range(2)]
                BTc = [sb(f"BTc{i}", [128, 4, 128], BF16) for i in range(2)]
                CTc = [sb(f"CTc{i}", [128, 4, 128], BF16) for i in range(2)]
                dtc = [sb(f"dtc{i}", [128, 64], F32) for i in range(2)]
                dA = sb("dA", [128, 32], F32)
                acs = sb("acs", [128, 32], F32)
                tot = sb("tot", [128, 32], F32)
                edec = sb("edec", [128, 32], F32)
                cdec = sb("cdec", [128, 32], F32)
                dst = sb("dst", [128, 32], F32)
                wd = sb("wd", [128, 32], F32)
                xdt = sb("xdt", [128, 2048], BF16)
                xdtd = sb("xdtd", [128, 2048], BF16)
                cbm = sb("cbm", [128, 4, 128], F32)
                rseg = [sb(f"rseg{i}", [128, 8, 128], F32) for i in range(2)]
                Eb = [sb(f"Eb{i}", [128, 8, 128], F32) for i in range(2)]
                MT = [sb(f"MT{i}", [128, 8, 128], BF16) for i in range(2)]
                tmp = sb("tmpy", [128, 512], F32)
                ydir = [sb(f"ydir{i}", [128, 2048], F32) for i in range(2)]
                pseg = [ps(f"pseg{i}", [128, 1024], F32) for i in range(2)]
                pY = ps("pY", [128, 512], F32)
                pYo = ps("pYo", [128, 512], F32)
                pST = ps("pST", [128, 512], F32)
                pcb = ps("pcb", [128, 512], F32)
                if final:
                    stage = [sb(f"stageZ{i}", [128, 2048], F32) for i in range(1)]
                    wz = sb("wz", [128, KC, 2048], BF16)
                    load_weight(stage, wz, win_d, KC, C_Z, 2048)
                    ybt = [sb(f"ybt{i}", [128, 2048], F32) for i in range(2)]
                    uTt = [sb(f"uTt{i}", [128, KC, 128], BF16) for i in range(2)]
                    t2 = sb("t2", [128, 2048], F32)
                    zs = sb("zs", [128, 2048], F32)
                    ynb = [sb(f"ynb{i}", [128, 2048], BF16) for i in range(2)]
                    D_bc = sb("D_bc", [128, 32], F32)
                    bcast_load(D_bc, dskip_d, 32)
                    ssdw_bc = sb("ssdw_bc", [128, 2048], F32)
                    bcast_load(ssdw_bc, ssdnw_d, 2048)
                    junk = sb("junkS", [128, 512], F32)
                    ssn = sb("ssn", [128, 4], F32)
                order = list(range(NTT)) if d == 0 else list(range(NTT - 1, -1, -1))
                gc = 0
                def loadC(cnt):
                    c = order[cnt]
                    n, t0, s = ntok(c), tok0(c), cnt % 2
                    S.dma("sp", out=xs[s][0:n, :], in_=xs_d[t0:t0 + n, :], writes=[("xs", s)])
                    S.dma("sp", out=Bt[s][0:n, :], in_=btok_d[t0:t0 + n, :], writes=[("Bt", s)])
                    S.dma("sp", out=BTc[s][:, :, 0:n], in_=bT_d[:, :, t0:t0 + n], writes=[("BTc", s)])
                    S.dma("sp", out=CTc[s][:, :, 0:n], in_=cT_d[:, :, t0:t0 + n], writes=[("CTc", s)])
                    S.dma("sp", out=dtc[s][0:n, :], in_=dt_d[t0:t0 + n, :], writes=[("dtc", s)])
                    if final and c > 0:
                        S.dma("sp", out=ybt[s][0:n, :], in_=yb_d[t0:t0 + n, :], writes=[("ybt", s)])
                        S.dma("sp", out=uTt[s][:, :, 0:n], in_=uT_d[:, :, t0:t0 + n], writes=[("uTt", s)])

                loadC(0)
                for cnt, c in enumerate(order):
                    n, t0, s = ntok(c), tok0(c), cnt % 2
                    if cnt + 1 < len(order):
                        loadC(cnt + 1)
                    dtd = dtc[s][0:n, d * 32:(d + 1) * 32]
                    S.op("dve", "tensor_tensor", out=dA[0:n, :], in0=dtd, in1=aneg[0:n, d * 32:(d + 1) * 32], op=ALU.mult,
                         reads=[("dtc", s), aneg.name], writes=["dA"])
                    S.op("pe", "matmul", out=pYo[0:n, 0:32], lhsT=mask_d[0:n, 0:n], rhs=dA[0:n, :], start=True, stop=True,
                         reads=["dA", "masks"], writes=["pYo"])
                    S.op("pe", "matmul", out=pYo[:, 32:64], lhsT=ones[0:n, :], rhs=dA[0:n, :], start=True, stop=True,
                         reads=["dA", "ones"], writes=["pYo"])
                    S.op("dve", "tensor_copy", out=acs[0:n, :], in_=pYo[0:n, 0:32], reads=["pYo"], writes=["acs"])
                    S.op("dve", "tensor_copy", out=tot[:, :], in_=pYo[:, 32:64], reads=["pYo"], writes=["tot"])
                    S.op("act", "activation", out=edec[0:n, :], in_=acs[0:n, :], func=AF.Exp, reads=["acs"], writes=["edec"])
                    S.op("act", "activation", out=cdec[:, :], in_=tot[:, :], func=AF.Exp, reads=["tot"], writes=["cdec"])
                    S.op("dve", "tensor_tensor", out=dst[0:n, :], in0=tot[0:n, :], in1=acs[0:n, :], op=ALU.subtract,
                         reads=["tot", "acs"], writes=["dst"])
                    S.op("act", "activation", out=dst[0:n, :], in_=dst[0:n, :], func=AF.Exp, reads=["dst"], writes=["dst"])
                    S.op("dve", "tensor_tensor", out=wd[0:n, :], in0=dtd, in1=dst[0:n, :], op=ALU.mult, reads=[("dtc", s), "dst"], writes=["wd"])
                    xs3 = xs[s][0:n, :].rearrange("p (h c) -> p h c", c=64)
                    S.op("dve", "tensor_tensor", out=xdt[0:n, :].rearrange("p (h c) -> p h c", c=64), in0=xs3,
                         in1=dtd.unsqueeze(2).to_broadcast([n, 32, 64]), op=ALU.mult, reads=[("xs", s), ("dtc", s)], writes=["xdt"])
                    S.op("dve", "tensor_tensor", out=xdtd[0:n, :].rearrange("p (h c) -> p h c", c=64), in0=xs3,
                         in1=wd[0:n, :].unsqueeze(2).to_broadcast([n, 32, 64]), op=ALU.mult, reads=[("xs", s), "wd"], writes=["xdtd"])
                    S.op("act", "copy", out=hTb[:], in_=hT[:], reads=[("hT", g) for g in range(4)], writes=["hTb"])
                    for g in range(4):
                        S.op("pe", "matmul", out=pcb[0:n, g * 128:g * 128 + n], lhsT=BTc[s][:, g, 0:n], rhs=CTc[s][:, g, 0:n], start=True, stop=True,
                             reads=[("BTc", s), ("CTc", s)], writes=["pcb"])
                    S.op("dve", "tensor_tensor", out=cbm[0:n, :, 0:n], in0=pcb[0:n, :].rearrange("p (g l) -> p g l", l=128)[:, :, 0:n],
                         in1=mask_d[0:n, 0:n].unsqueeze(1).to_broadcast([n, 4, n]), op=ALU.mult, reads=["pcb", "masks"], writes=["cbm"])
                    for g in range(4):
                        sb2 = gc % 2
                        gc += 1
                        S.op("dve", "tensor_tensor", out=rseg[sb2][0:n, :, 0:n], in0=dA[0:n, g * 8:(g + 1) * 8].unsqueeze(2).to_broadcast([n, 8, n]),
                             in1=mask_d[0:n, 0:n].unsqueeze(1).to_broadcast([n, 8, n]), op=ALU.mult, reads=["dA", "masks"], writes=[("rseg", sb2)])
                        for half in range(2):
                            pv = pseg[sb2][0:n, half * 512:(half + 1) * 512].rearrange("p (j l) -> p j l", l=128)[:, :, 0:n]
                            if n == 128:
                                S.op("pe", "matmul", out=pseg[sb2][0:n, half * 512:(half + 1) * 512], lhsT=L_d[0:n, 0:n],
                                     rhs=rseg[sb2][0:n, half * 4:(half + 1) * 4, :].rearrange("p j l -> p (j l)"), start=True, stop=True,
                                     reads=[("rseg", sb2), "masks"], writes=[("pseg", sb2, half)])
                            else:
                                for j in range(4):
                                    jj = half * 4 + j
                                    S.op("pe", "matmul", out=pseg[sb2][0:n, jj * 128:jj * 128 + n], lhsT=L_d[0:n, 0:n],
                                         rhs=rseg[sb2][0:n, jj, 0:n], start=True, stop=True,
                                         reads=[("rseg", sb2), "masks"], writes=[("pseg", sb2, half)])
                            S.op("act", "activation", out=Eb[sb2][0:n, half * 4:(half + 1) * 4, 0:n], in_=pv, func=AF.Exp,
                                 reads=[("pseg", sb2, half)], writes=[("Eb", sb2, half)])
                        S.op("dve", "tensor_tensor", out=MT[sb2][0:n, :, 0:n], in0=Eb[sb2][0:n, :, 0:n],
                             in1=cbm[0:n, g, 0:n].unsqueeze(1).to_broadcast([n, 8, n]), op=ALU.mult,
                             reads=[("Eb", sb2, 0), ("Eb", sb2, 1), "cbm"], writes=[("MT", sb2)])
                        for j in range(8):
                            S.op("pe", "matmul", out=pY[0:n, j * 64:(j + 1) * 64], lhsT=MT[sb2][0:n, j, 0:n], rhs=xdt[0:n, (g * 8 + j) * 64:(g * 8 + j + 1) * 64],
                                 start=True, stop=True, reads=[("MT", sb2), "xdt"], writes=["pY"])
                        S.op("pe", "matmul", out=pYo[0:n, 0:512], lhsT=CTc[s][:, g, 0:n], rhs=hTb[:, g * 512:(g + 1) * 512], start=True, stop=True,
                             reads=[("CTc", s), "hTb"], writes=["pYo"])
                        S.op("dve", "tensor_tensor", out=tmp[0:n, :].rearrange("p (j c) -> p j c", c=64), in0=pYo[0:n, :].rearrange("p (j c) -> p j c", c=64),
                             in1=edec[0:n, g * 8:(g + 1) * 8].unsqueeze(2).to_broadcast([n, 8, 64]), op=ALU.mult, reads=["pYo", "edec"], writes=["tmpy"])
                        S.op("dve", "tensor_tensor", out=ydir[s][0:n, g * 512:(g + 1) * 512], in0=tmp[0:n, :], in1=pY[0:n, :], op=ALU.add,
                             reads=["tmpy", "pY"], writes=[("ydir", s)])
                        S.op("pe", "matmul", out=pST[:, 0:512], lhsT=Bt[s][0:n, g * 128:(g + 1) * 128], rhs=xdtd[0:n, g * 512:(g + 1) * 512], start=True, stop=True,
                             reads=[("Bt", s), "xdtd"], writes=["pST"])
                        hg = hT[:, g * 512:(g + 1) * 512]
                        S.op("dve", "tensor_tensor", out=hg.rearrange("p (j c) -> p j c", c=64), in0=hg.rearrange("p (j c) -> p j c", c=64),
                             in1=cdec[:, g * 8:(g + 1) * 8].unsqueeze(2).to_broadcast([128, 8, 64]), op=ALU.mult, reads=[("hT", g), "cdec"], writes=[("hT", g)])
                        S.op("dve", "tensor_tensor", out=hg, in0=hg, in1=pST[:, 0:512], op=ALU.add, reads=[("hT", g), "pST"], writes=[("hT", g)])
                    if not final:
                        S.dma(STQ, out=yb_d[t0:t0 + n, :], in_=ydir[s][0:n, :], reads=[("ydir", s)], writes=[("yb_d", c)])
                    elif c > 0:
                        y = ydir[s][0:n, :]
                        S.op("dve", "tensor_tensor", out=y, in0=y, in1=ybt[s][0:n, :], op=ALU.add, reads=[("ydir", s), ("ybt", s)], writes=[("ydir", s)])
                        S.op("dve", "tensor_tensor", out=t2[0:n, :].rearrange("p (h c) -> p h c", c=64), in0=xs3,
                             in1=D_bc[0:n, :].unsqueeze(2).to_broadcast([n, 32, 64]), op=ALU.mult, reads=[("xs", s), D_bc.name], writes=["t2"])
                        S.op("dve", "tensor_tensor", out=y, in0=y, in1=t2[0:n, :], op=ALU.add, reads=[("ydir", s), "t2"], writes=[("ydir", s)])
                        for q in range(4):
                            pz = pseg[q // 2][0:n, (q % 2) * 512:(q % 2 + 1) * 512]
                            for k in range(KC):
                                S.op("pe", "matmul", out=pz, lhsT=uTt[s][:, k, 0:n], rhs=wz[:, k, q * 512:(q + 1) * 512], start=(k == 0), stop=(k == KC - 1),
                                     reads=[("uTt", s), wz.name], writes=[("pseg", q // 2, q % 2)])
                            S.op("act", "activation", out=zs[0:n, q * 512:(q + 1) * 512], in_=pz, func=AF.Silu, reads=[("pseg", q // 2, q % 2)], writes=[("zs", q)])
                        S.op("dve", "tensor_tensor", out=y, in0=y, in1=zs[0:n, :], op=ALU.mult, reads=[("ydir", s)] + [("zs", q) for q in range(4)], writes=[("ydir", s)])
                        for g4 in range(4):
                            S.op("act", "activation", out=junk[0:n, :], in_=ydir[s][0:n, g4 * 512:(g4 + 1) * 512], func=AF.Square, accum_out=ssn[0:n, g4:g4 + 1],
                                 reads=[("ydir", s)], writes=["junkS", "ssn"])
                        S.op("act", "activation", out=ssn[0:n, :], in_=ssn[0:n, :], func=AF.Sqrt, scale=1.0 / 512, bias=EPS, reads=["ssn"], writes=["ssn"])
                        S.op("dve", "reciprocal", out=ssn[0:n, :], in_=ssn[0:n, :], reads=["ssn"], writes=["ssn"])
                        for g4 in range(4):
                            S.op("dve", "scalar_tensor_tensor", out=ynb[s][0:n, g4 * 512:(g4 + 1) * 512], in0=ydir[s][0:n, g4 * 512:(g4 + 1) * 512],
                                 scalar=ssn[0:n, g4:g4 + 1], in1=ssdw_bc[0:n, g4 * 512:(g4 + 1) * 512], op0=ALU.mult, op1=ALU.mult,
                                 reads=[("ydir", s), "ssn", ssdw_bc.name], writes=[("ynb", s)])
                        S.dma(STQ, out=yn_d[t0:t0 + n, :], in_=ynb[s][0:n, :], reads=[("ynb", s)], writes=[("yn_d", c)])
                S.flush()

        if "1" in phases:
            ssd_pass(1, False)
        if "2" in phases:
            ssd_pass(0, True)

        if "F" in phases:
            with contextlib.ExitStack() as esF:
                sb, ps = mk(esF)
                stage = [sb("stageF0", [128, 2048], F32)]
                wg = sb("wg", [128, KC, 1024], BF16)
                wmg = sb("wmg", [128, KC, 2048], BF16)
                wap = sb("wap", [128, KC, 1024], BF16)
                wsp = sb("wsp", [128, 16, 1024], BF16)
                wout = sb("wout", [128, KC, 1024], BF16)
                load_weight(stage, wg, win_d, KC, C_G, 1024)
                load_weight(stage, wmg, win_d, KC, C_MG, 2048)
                load_weight(stage, wap, wap_d, KC, 0, 1024)
                load_weight(stage, wsp, wsp_d, 16, 0, 1024)
                load_weight(stage, wout, wout_d, KC, 0, 1024)
                fnw_bc = sb("fnw_bc", [128, D], F32)
                bcast_load(fnw_bc, fnw_d, D)
                uTt = [sb(f"uTtF{i}", [128, KC, 128], BF16) for i in range(2)]
                ogt = [sb(f"ogt{i}", [128, D], BF16) for i in range(2)]
                ynt = [sb(f"ynt{i}", [128, 2048], BF16) for i in range(2)]
                xt = [sb(f"xtF{i}", [128, D], F32) for i in range(2)]
                gs = sb("gs", [128, D], F32)
                ogg = sb("ogg", [128, D], BF16)
                ogT = sb("ogT", [128, KC, 128], BF16)
                ynT = sb("ynT", [128, 16, 128], BF16)
                s0 = sb("s0", [128, 512], F32)
                s1 = sb("s1", [128, 512], F32)
                m0 = sb("m0", [128, 512], F32)
                m1 = sb("m1", [128, 512], F32)
                mix = sb("mix", [128, D], BF16)
                mixT = sb("mixT", [128, KC, 128], BF16)
                hres = sb("hres", [128, D], F32)
                yo = [sb(f"yo{i}", [128, D], F32) for i in range(2)]
                junk = sb("junkF", [128, D], F32)
                ssf = sb("ssf", [128, 1], F32)
                pg = [ps(f"pg{i}", [128, 512], F32) for i in range(2)]
                ptr = ps("ptrF", [128, KC, 128], BF16)
                pa = ps("pa", [128, 512], F32)
                psy = ps("psy", [128, 512], F32)
                pm0 = ps("pm0", [128, 512], F32)
                pm1 = ps("pm1", [128, 512], F32)
                def loadF(i):
                    t0, s = tok0(i), i % 2
                    n = 128
                    S.dma("sp", out=uTt[s][:, :, :], in_=uT_d[:, :, t0:t0 + n], writes=[("uTt", s)])
                    S.dma("sp", out=ogt[s][:, :], in_=og_d[t0:t0 + n, :], writes=[("ogt", s)])
                    S.dma("sp", out=ynt[s][:, :], in_=yn_d[t0:t0 + n, :], writes=[("ynt", s)])
                    S.dma("sp", out=xt[s][:, :], in_=x_d[(i - 1) * 128:i * 128, :], writes=[("xt", s)])

                loadF(1)
                for i in range(1, NTT):
                    t0, s = tok0(i), i % 2
                    n = 128
                    if i + 1 < NTT:
                        loadF(i + 1)
                    for hf in range(2):
                        for k in range(KC):
                            S.op("pe", "matmul", out=pg[hf][:, :], lhsT=uTt[s][:, k, :], rhs=wg[:, k, hf * 512:(hf + 1) * 512], start=(k == 0), stop=(k == KC - 1),
                                 reads=[("uTt", s), "wg"], writes=[("pg", hf)])
                        S.op("act", "activation", out=gs[:, hf * 512:(hf + 1) * 512], in_=pg[hf][:, :], func=AF.Silu, reads=[("pg", hf)], writes=[("gs", hf)])
                    S.op("dve", "tensor_tensor", out=ogg[:, :], in0=ogt[s][:, :], in1=gs[:, :], op=ALU.mult, reads=[("ogt", s), ("gs", 0), ("gs", 1)], writes=["ogg"])
                    for k in range(KC):
                        S.op("pe", "transpose", out=ptr[:, k, :], in_=ogg[:, k * 128:(k + 1) * 128], identity=idb[:], reads=["ogg", "idb"], writes=["ptr"])
                    S.op("act", "copy", out=ogT[:, :, :], in_=ptr[:, :, :], reads=["ptr"], writes=["ogT"])
                    for r2 in range(2):
                        for k in range(KC):
                            kk = r2 * KC + k
                            S.op("pe", "transpose", out=ptr[:, k, :], in_=ynt[s][:, kk * 128:(kk + 1) * 128], identity=idb[:], reads=[("ynt", s), "idb"], writes=["ptr"])
                        S.op("dve" if r2 else "act", "tensor_copy" if r2 else "copy", out=ynT[:, r2 * KC:(r2 + 1) * KC, :], in_=ptr[:, :, :], reads=["ptr"], writes=[("ynT", r2)])
                    for hf in range(2):
                        cs_ = slice(hf * 512, (hf + 1) * 512)
                        for k in range(KC):
                            S.op("pe", "matmul", out=pa[:, :], lhsT=ogT[:, k, :], rhs=wap[:, k, cs_], start=(k == 0), stop=(k == KC - 1), reads=["ogT", "wap"], writes=["pa"])
                        for k in range(16):
                            S.op("pe", "matmul", out=psy[:, :], lhsT=ynT[:, k, :], rhs=wsp[:, k, cs_], start=(k == 0), stop=(k == 15), reads=[("ynT", 0), ("ynT", 1), "wsp"], writes=["psy"])
                        for k in range(KC):
                            S.op("pe", "matmul", out=pm0[:, :], lhsT=uTt[s][:, k, :], rhs=wmg[:, k, hf * 512:(hf + 1) * 512], start=(k == 0), stop=(k == KC - 1), reads=[("uTt", s), "wmg"], writes=["pm0"])
                        for k in range(KC):
                            S.op("pe", "matmul", out=pm1[:, :], lhsT=uTt[s][:, k, :], rhs=wmg[:, k, 1024 + hf * 512:1024 + (hf + 1) * 512], start=(k == 0), stop=(k == KC - 1), reads=[("uTt", s), "wmg"], writes=["pm1"])
                        S.op("act", "activation", out=s0[:, :], in_=pm0[:, :], func=AF.Sigmoid, reads=["pm0"], writes=["s0"])
                        S.op("act", "activation", out=s1[:, :], in_=pm1[:, :], func=AF.Sigmoid, reads=["pm1"], writes=["s1"])
                        S.op("dve", "tensor_tensor", out=m0[:, :], in0=s0[:, :], in1=pa[:, :], op=ALU.mult, reads=["s0", "pa"], writes=["m0"])
                        S.op("dve", "tensor_tensor", out=m1[:, :], in0=s1[:, :], in1=psy[:, :], op=ALU.mult, reads=["s1", "psy"], writes=["m1"])
                        S.op("dve", "tensor_tensor", out=mix[:, cs_], in0=m0[:, :], in1=m1[:, :], op=ALU.add, reads=["m0", "m1"], writes=[("mix", hf)])
                    for k in range(KC):
                        S.op("pe", "transpose", out=ptr[:, k, :], in_=mix[:, k * 128:(k + 1) * 128], identity=idb[:], reads=[("mix", 0), ("mix", 1), "idb"], writes=["ptr"])
                    S.op("act", "copy", out=mixT[:, :, :], in_=ptr[:, :, :], reads=["ptr"], writes=["mixT"])
                    for hf in range(2):
                        for k in range(KC):
                            S.op("pe", "matmul", out=pg[hf][:, :], lhsT=mixT[:, k, :], rhs=wout[:, k, hf * 512:(hf + 1) * 512], start=(k == 0), stop=(k == KC - 1),
                                 reads=["mixT", "wout"], writes=[("pg", hf)])
                        S.op("dve", "tensor_tensor", out=hres[:, hf * 512:(hf + 1) * 512], in0=xt[s][:, hf * 512:(hf + 1) * 512], in1=pg[hf][:, :], op=ALU.add,
                             reads=[("xt", s), ("pg", hf)], writes=[("hres", hf)])
                    S.op("act", "activation", out=junk[:, :], in_=hres[:, :], func=AF.Square, accum_out=ssf[:, 0:1], reads=[("hres", 0), ("hres", 1)], writes=["junkF", "ssf"])
                    S.op("act", "activation", out=ssf[:, 0:1], in_=ssf[:, 0:1], func=AF.Sqrt, scale=1.0 / D, bias=EPS, reads=["ssf"], writes=["ssf"])
                    S.op("dve", "reciprocal", out=ssf[:, 0:1], in_=ssf[:, 0:1], reads=["ssf"], writes=["ssf"])
                    S.op("dve", "scalar_tensor_tensor", out=yo[s][:, :], in0=hres[:, :], scalar=ssf[:, 0:1], in1=fnw_bc[:, :], op0=ALU.mult, op1=ALU.mult,
                         reads=[("hres", 0), ("hres", 1), "ssf", "fnw_bc"], writes=[("yo", s)])
                    S.dma(STQ, out=out_d[(i - 1) * 128:i * 128, :], in_=yo[s][:, :], reads=[("yo", s)], writes=[("out", i)])
                S.flush()
        return nc


def host_consts(L):
    pos = np.arange(L, dtype=np.float32)
    inv_freq = (np.float32(10000.0) ** (-(np.arange(0, 32, 2, dtype=np.float32)) / np.float32(32))).astype(np.float32)
    ang = (pos[:, None] * inv_freq[None, :]).astype(np.float32)
    cos, sin = np.cos(ang).astype(np.float32), np.sin(ang).astype(np.float32)
    cs_tok = np.concatenate([cos, sin], axis=1).astype(np.float32)
    cos2 = np.ascontiguousarray(np.concatenate([cos, cos], axis=1).T)
    sin2 = np.ascontiguousarray(np.concatenate([sin, sin], axis=1).T)
    r = np.arange(128)[:, None]
    c = np.arange(128)[None, :]
    masks = np.stack([(r <= c), (r >= c), (r > c), (r < c)]).astype(np.float32)
    return dict(ident=np.eye(128, dtype=np.float32), cs_tok=cs_tok, cos2=cos2, sin2=sin2, masks=masks)


def make_in_maps(inputs, NT):
    L = NMETA + 128 * NT
    f = lambda a: np.ascontiguousarray(np.asarray(a, dtype=np.float32))
    x = f(inputs["x"])
    B = x.shape[0]
    shared = dict(
        meta=f(inputs["meta_tokens"]),
        norm_w=f(inputs["norm_w"]).reshape(1, D),
        w_in=f(inputs["w_in"]).reshape(D, C_END),
        lat_gains=np.concatenate([f(inputs["q_norm_w"]).reshape(1, 384), f(inputs["kv_norm_w"]).reshape(1, 256)], axis=1),
        w_uq=f(inputs["w_uq"]).reshape(384, 1536),
        w_ukv=f(inputs["w_ukv"]).reshape(256, 2048),
        w_attn_proj=f(inputs["w_attn_proj"]).reshape(D, D),
        convw_fm=np.ascontiguousarray(f(inputs["conv_w"]).reshape(5, 24, 128).transpose(2, 1, 0)),
        convb_fm=np.ascontiguousarray(f(inputs["conv_b"]).reshape(24, 128).T),
        dt_bias=f(inputs["dt_bias"]).reshape(1, 64),
        a_log=f(inputs["a_log"]).reshape(1, 64),
        d_skip=f(inputs["d_skip"]).reshape(1, 32),
        ssd_norm_w=f(inputs["ssd_norm_w"]).reshape(1, 2048),
        w_ssd_proj=f(inputs["w_ssd_proj"]).reshape(2048, D),
        w_out=f(inputs["w_out"]).reshape(D, D),
        final_norm_w=f(inputs["final_norm_w"]).reshape(1, D),
    )
    shared.update(host_consts(L))
    maps = []
    for b in range(B):
        m = dict(shared)
        m["x"] = np.ascontiguousarray(x[b])
        maps.append(m)
    return maps


def kernel(**inputs):
    x = np.asarray(inputs["x"])
    B, SEQ, _ = x.shape
    NT = SEQ // 128
    nc = build_nc(NT)
    in_maps = make_in_maps(inputs, NT)
    res = run_bass_kernel_spmd(nc, in_maps, core_ids=list(range(B)))
    return np.stack([np.asarray(r["out"], dtype=np.float32) for r in res.results], axis=0)
```
